# Optimizing a Trainium2 kernel written in Bass

```python
import jax, jax.numpy as jnp
from jax import lax
import numpy as np

D_MODEL = 1024
BATCH = 2
SEQ = 16384
DEPTH = 4

HEAD_DIM = 64
A_Q_HEADS = 8
A_KV_HEADS = 2
A_WINDOW = 128
B_GROUPS = ((128, 1), (512, 4), (2048, 16))
B_HEADS_PER_GROUP = 4
B_HEADS = B_HEADS_PER_GROUP * len(B_GROUPS)
N_ATTN_HEADS = A_Q_HEADS + B_HEADS
BLOCK = 128
A_Q_W = A_Q_HEADS * HEAD_DIM
A_KV_W = A_KV_HEADS * HEAD_DIM
B_W = B_HEADS * HEAD_DIM
B_OUT_W = B_HEADS_PER_GROUP * HEAD_DIM
IN_SPLITS = (A_Q_W, A_KV_W, A_KV_W, B_W, B_W, B_W, D_MODEL, D_MODEL)
IN_W = sum(IN_SPLITS)
D_FF = ((8 * D_MODEL + 3 * 256 - 1) // (3 * 256)) * 256
DN_ALPHA = (2 * DEPTH) ** 0.25
DN_BETA = (8 * DEPTH) ** -0.25
LN_EPS = 1e-5
NEG_INF = -1e30

kernel_name = "hybrid_swa_sink_dilated_gated_deepnorm"


def layer_norm(x, g, b):
    xf = x.astype(jnp.float32)
    mu = xf.mean(-1, keepdims=True)
    var = jnp.square(xf - mu).mean(-1, keepdims=True)
    y = (xf - mu) * lax.rsqrt(var + LN_EPS)
    return (y * g.astype(jnp.float32) + b.astype(jnp.float32)).astype(x.dtype)


def alibi_slopes(n):
    return jnp.exp2(-8.0 * jnp.arange(1, n + 1, dtype=jnp.float32) / n)


def banded_attention(q, k, v, slopes, max_dist, stride, sinks=None):
    bt, L, H, dh = q.shape
    hkv = k.shape[2]
    G = H // hkv
    nb = -(-L // BLOCK)
    Lp = nb * BLOCK
    q = jnp.pad(q, ((0, 0), (0, Lp - L), (0, 0), (0, 0)))
    kv_pad = ((0, 0), (BLOCK, Lp - L), (0, 0), (0, 0))
    k = jnp.pad(k, kv_pad).reshape(bt, nb + 1, BLOCK, hkv, dh)
    v = jnp.pad(v, kv_pad).reshape(bt, nb + 1, BLOCK, hkv, dh)
    kw = jnp.concatenate([k[:, :-1], k[:, 1:]], axis=2)
    vw = jnp.concatenate([v[:, :-1], v[:, 1:]], axis=2)
    qb = q.reshape(bt, nb, BLOCK, hkv, G, dh)
    s = jnp.einsum('bnqhgd,bnshd->bnhgqs', qb, kw,
                   preferred_element_type=jnp.float32) * (dh ** -0.5)
    qi = jnp.arange(BLOCK)[:, None]
    sj = jnp.arange(2 * BLOCK)[None, :]
    dist = qi + BLOCK - sj
    kpos = jnp.arange(nb)[:, None] * BLOCK + jnp.arange(2 * BLOCK)[None, :] - BLOCK
    valid = ((dist >= 0) & (dist <= max_dist))[None] & (kpos >= 0)[:, None, :]
    bias = -(slopes.astype(jnp.float32).reshape(hkv, G, 1, 1)
             * (dist * stride).astype(jnp.float32))
    s = jnp.where(valid[None, :, None, None], s + bias, NEG_INF)
    m = s.max(-1)
    if sinks is not None:
        sink = sinks.astype(jnp.float32).reshape(1, 1, hkv, G, 1)
        m = jnp.maximum(m, sink)
    e = jnp.exp(s - m[..., None])
    den = e.sum(-1)
    if sinks is not None:
        den = den + jnp.exp(sink - m)
    lse = m + jnp.log(den)
    p = (e / den[..., None]).astype(v.dtype)
    o = jnp.einsum('bnhgqs,bnshd->bnqhgd', p, vw).reshape(bt, Lp, H, dh)[:, :L]
    lse = lse.transpose(0, 1, 4, 2, 3).reshape(bt, Lp, H)[:, :L]
    return o, lse


def dilated_group(q, k, v, slopes, window, dilation):
    b, S, h, dh = q.shape
    n = S // dilation

    def fold(t):
        return t.reshape(b, n, dilation, h, dh).transpose(0, 2, 1, 3, 4).reshape(b * dilation, n, h, dh)

    o, lse = banded_attention(fold(q), fold(k), fold(v), slopes, window // dilation, dilation)
    o = o.reshape(b, dilation, n, h, dh).transpose(0, 2, 1, 3, 4).reshape(b, S, h, dh)
    lse = lse.reshape(b, dilation, n, h).transpose(0, 2, 1, 3).reshape(b, S, h)
    return o, lse


def token_mixer(u, w_in, sinks, w_a, w_b, w_o):
    b, S, _ = u.shape
    idx = list(np.cumsum(IN_SPLITS)[:-1])
    qa, ka, va, qb, kb, vb, ga, gb = jnp.split(u @ w_in, idx, axis=-1)
    slopes = alibi_slopes(N_ATTN_HEADS)
    ya, _ = banded_attention(qa.reshape(b, S, A_Q_HEADS, HEAD_DIM),
                             ka.reshape(b, S, A_KV_HEADS, HEAD_DIM),
                             va.reshape(b, S, A_KV_HEADS, HEAD_DIM),
                             slopes[:A_Q_HEADS], A_WINDOW - 1, 1, sinks)
    ya = ya.reshape(b, S, A_Q_W)
    gshape = (b, S, len(B_GROUPS), B_HEADS_PER_GROUP, HEAD_DIM)
    qb, kb, vb = qb.reshape(gshape), kb.reshape(gshape), vb.reshape(gshape)
    outs, lses = [], []
    for g, (window, dilation) in enumerate(B_GROUPS):
        lo = A_Q_HEADS + g * B_HEADS_PER_GROUP
        o, l = dilated_group(qb[:, :, g], kb[:, :, g], vb[:, :, g],
                             slopes[lo:lo + B_HEADS_PER_GROUP], window, dilation)
        outs.append(o)
        lses.append(l)
    wts = jax.nn.softmax(jnp.stack(lses), axis=0)
    yb = (jnp.stack(outs) * wts[..., None].astype(u.dtype)).sum(0).reshape(b, S, B_OUT_W)
    merged = jax.nn.sigmoid(ga) * (ya @ w_a) + jax.nn.sigmoid(gb) * (yb @ w_b)
    return merged @ w_o


def swiglu(u, w_gate, w_up, w_down):
    return (jax.nn.silu(u @ w_gate) * (u @ w_up)) @ w_down


def setup_inputs(seed: int = 0) -> dict:
    key = jax.random.key(seed)
    ks = jax.random.split(key, 20)
    nrm = lambda k, shape, s: jax.random.normal(k, shape, jnp.float32) * s
    L, D = DEPTH, D_MODEL
    return {
        "x": nrm(ks[0], (BATCH, SEQ, D), 1.0),
        "c": nrm(ks[1], (BATCH, D), 1.0),
        "w_ada": nrm(ks[2], (L, D, 6 * D), 0.5 * D ** -0.5),
        "b_ada": nrm(ks[3], (L, 6 * D), 0.02),
        "w_in": nrm(ks[4], (L, D, IN_W), D ** -0.5),
        "sinks": nrm(ks[5], (L, A_Q_HEADS), 0.5),
        "w_a": nrm(ks[6], (L, A_Q_W, D), A_Q_W ** -0.5),
        "w_b": nrm(ks[7], (L, B_OUT_W, D), B_OUT_W ** -0.5),
        "w_o": nrm(ks[8], (L, D, D), DN_BETA * D ** -0.5),
        "ln1_g": 1.0 + nrm(ks[9], (L, D), 0.02),
        "ln1_b": nrm(ks[10], (L, D), 0.02),
        "w_gate": nrm(ks[11], (L, D, D_FF), D ** -0.5),
        "w_up": nrm(ks[12], (L, D, D_FF), D ** -0.5),
        "w_down": nrm(ks[13], (L, D_FF, D), DN_BETA * D_FF ** -0.5),
        "ln2_g": 1.0 + nrm(ks[14], (L, D), 0.02),
        "ln2_b": nrm(ks[15], (L, D), 0.02),
    }


def reference(x, c, w_ada, b_ada, w_in, sinks, w_a, w_b, w_o, ln1_g, ln1_b,
              w_gate, w_up, w_down, ln2_g, ln2_b):
    sc = jax.nn.silu(c)
    for l in range(DEPTH):
        mod = (sc @ w_ada[l] + b_ada[l])[:, None, :]
        sh1, s1, g1, sh2, s2, g2 = jnp.split(mod, 6, axis=-1)
        u = x * (1 + s1) + sh1
        x = layer_norm(DN_ALPHA * x + g1 * token_mixer(u, w_in[l], sinks[l], w_a[l], w_b[l], w_o[l]),
                       ln1_g[l], ln1_b[l])
        u = x * (1 + s2) + sh2
        x = layer_norm(DN_ALPHA * x + g2 * swiglu(u, w_gate[l], w_up[l], w_down[l]),
                       ln2_g[l], ln2_b[l])
    return x
```

```python
import numpy as np
from contextlib import ExitStack
import concourse.bass as bass
import concourse.mybir as mybir
from concourse.bass_utils import run_bass_kernel_spmd

F32 = mybir.dt.float32
BF16 = mybir.dt.bfloat16
AF = mybir.ActivationFunctionType
ALU = mybir.AluOpType

P = 128
D = 1024
T_OWN = 4096
HALO = 2048
TG = 2048
DFF = 2816
NF = 22
DEPTH = 4
ALPHA = float((2 * DEPTH) ** 0.25)
EPS = 1e-5
N_CORES = 8

QA0, KA0, VA0, QB0, KB0, VB0, GA0, GB0 = 0, 512, 640, 768, 1536, 2304, 3072, 4096
B_GROUPS = ((128, 1), (512, 4), (2048, 16))

UNITS = []
for u in range(4):
    UNITS.append(dict(kind="A", idx=u, d=1, maxdist=127, heads=(u, u + 4),
                      slope_idx=(u, u + 4)))
for hp in range(2):
    for g in range(3):
        UNITS.append(dict(kind="B", idx=hp, g=g, d=B_GROUPS[g][1], maxdist=128,
                          slope_idx=(8 + 4 * g + 2 * hp, 8 + 4 * g + 2 * hp + 1)))
WIN_BLOCKS = []
WIN_BLOCKS.append(list(range(KA0, KA0 + 128)))
WIN_BLOCKS.append(list(range(VA0, VA0 + 128)))
for u in range(4):
    WIN_BLOCKS.append(list(range(QA0 + 64 * u, QA0 + 64 * u + 64)) +
                      list(range(QA0 + 64 * (u + 4), QA0 + 64 * (u + 4) + 64)))
for hp in range(2):
    for g in range(3):
        o = 256 * g + 128 * hp
        WIN_BLOCKS.append(list(range(QB0 + o, QB0 + o + 128)))
        WIN_BLOCKS.append(list(range(KB0 + o, KB0 + o + 128)))
        WIN_BLOCKS.append(list(range(VB0 + o, VB0 + o + 128)))
for c in range(8):
    WIN_BLOCKS.append(list(range(GA0 + 128 * c, GA0 + 128 * c + 128)))
    WIN_BLOCKS.append(list(range(GB0 + 128 * c, GB0 + 128 * c + 128)))
NBLK = len(WIN_BLOCKS)


class Tile:
    __slots__ = ("name", "w", "r", "dsem", "dcnt")

    def __init__(self, name):
        self.name = name
        self.w = None
        self.r = []
        self.dsem = None
        self.dcnt = 0


class Tracker:
    def __init__(self, nc, stack):
        self.nc = nc
        self.stack = stack
        self.eng = {"pe": nc.tensor, "act": nc.scalar, "dve": nc.vector,
                    "pool": nc.gpsimd, "sp": nc.sync}
        self.sems = {}
        self.cnt = {}
        self.seen = {e: {} for e in self.eng}
        for e in self.eng:
            self.sems[e] = stack.enter_context(nc.semaphore("sem_" + e))
            self.cnt[e] = 0
        self.dma_tiles = []
        self.nbank = 0

    def _wait(self, e, deps):
        best = {}
        for (k, v) in deps:
            if k == e and e == "pe":
                continue
            if v > best.get(k, 0):
                best[k] = v
        for k, v in best.items():
            if self.seen[e].get(k, 0) >= v:
                continue
            self.eng[e].wait_ge(self.sems[k], v)
            self.seen[e][k] = v

    def _deps(self, e, reads, writes):
        deps = []
        for t in reads:
            if t.w is not None:
                deps.append(t.w)
        for t in writes:
            if t.w is not None:
                deps.append(t.w)
            deps.extend(t.r)
        return deps

    def _commit(self, ev, reads, writes):
        for t in reads:
            t.r = [x for x in t.r if x[0] != ev[0]] + [ev]
        for t in writes:
            t.w = ev
            t.r = []

    def op(self, e, fn, reads=(), writes=()):
        self._wait(e, self._deps(e, reads, writes))
        ins = fn()
        self.cnt[e] += 1
        ins.then_inc(self.sems[e], 1)
        self.seen[e][e] = max(self.seen[e].get(e, 0), 0)
        self._commit((e, self.cnt[e]), reads, writes)

    def dma(self, q, out, in_, sb_tile, reads=(), writes=()):
        self._wait(q, self._deps("dma", reads, writes))
        if sb_tile.dsem is None:
            key = "dma:" + sb_tile.name
            assert key not in self.sems, key
            self.sems[key] = self.stack.enter_context(self.nc.semaphore("d_" + sb_tile.name))
            sb_tile.dsem = key
            self.dma_tiles.append(sb_tile)
        ins = self.eng[q].dma_start(out=out, in_=in_)
        sb_tile.dcnt += 16
        ins.then_inc(self.sems[sb_tile.dsem], 16)
        self._commit((sb_tile.dsem, sb_tile.dcnt), reads, writes)

    def barrier(self):
        evs = [(e, self.cnt[e]) for e in self.eng if self.cnt[e] > 0]
        evs += [(t.dsem, t.dcnt) for t in self.dma_tiles if t.dcnt > 0]
        for e in self.eng:
            self._wait(e, [ev for ev in evs if ev[0] != e])


class _Stop(Exception):
    pass


def build_program(stop=None):
    import os
    stop = stop or int(os.environ.get("KSTOP", "0"))
    nc = bass.Bass("TRN2", target_bir_lowering=False)
    try:
        _build(nc, stop)
    except _Stop:
        pass
    return nc


def _build(nc, stop):

    def din(name, shape, dt=F32):
        return nc.dram_tensor(name, shape, dt, kind="ExternalInput").ap()

    x_in = din("x_in", [T_OWN, D])
    x_halo = din("x_halo", [HALO, D])
    hv_d = din("hv", [P, 1])
    c_col = din("c_col", [P, 8])
    wada = din("wada", [P, 6 * 8 * 1024])
    bada_col = din("bada_col", [P, 48])
    bada_row = din("bada_row", [1, 6144])
    win = din("win", [P, NBLK * 1024])
    wa_d = din("wa", [64, 8 * 1024])
    wb_d = din("wb", [64, 4 * 1024])
    wo_d = din("wo", [P, 8 * 1024])
    wg_d = din("wg", [P, 8 * DFF])
    wu_d = din("wu", [P, 8 * DFF])
    wd_d = din("wd", [P, NF * 1024])
    sinks_d = din("sinks", [1, 8])
    ln1g_d = din("ln1g", [1, D])
    ln1b_d = din("ln1b", [1, D])
    ln2g_d = din("ln2g", [1, D])
    ln2b_d = din("ln2b", [1, D])
    etab_d = din("etab", [P, 10 * 512])
    ident_d = din("ident", [P, P])
    x_out = nc.dram_tensor("x_out", [T_OWN, D], F32, kind="ExternalOutput").ap()
    x1_d = nc.dram_tensor("x1s", [T_OWN, D], F32, kind="Internal").ap()

    with ExitStack() as top:
        T = Tracker(nc, top)
        uid = [0]

        def chk(n):
            if stop == n:
                T.barrier()
                raise _Stop()

        def sb(stack, name, shape, dt):
            uid[0] += 1
            nm = f"{name}_{uid[0]}"
            t = stack.enter_context(nc.sbuf_tensor(nm, shape, dt))
            return t, Tile(nm)

        ps = []
        pst = []
        for i in range(8):
            ps.append(top.enter_context(nc.psum_tensor(f"ps{i}", [P, 512], F32)))
            pst.append(Tile(f"ps{i}"))

        def bank():
            b = T.nbank % 8
            T.nbank += 1
            return b

        dram_x1 = Tile("dram_x1")
        dram_in = Tile("dram_in")
        dram_out = Tile("dram_out")

        ident, ident_t = sb(top, "ident", [P, P], F32)
        T.dma("sp", ident[:], ident_d[:, :], ident_t, writes=[ident_t])
        hv, hv_t = sb(top, "hv", [P, 1], F32)
        T.dma("sp", hv[:], hv_d[:, :], hv_t, writes=[hv_t])
        modcol, modcol_t = sb(top, "modcol", [P, 48], F32)
        bcol, bcol_t = sb(top, "bcol", [P, 48], F32)
        T.dma("sp", bcol[:], bada_col[:, :], bcol_t, writes=[bcol_t])
        esink, esink_t = sb(top, "esink", [P, 8], F32)
        T.dma("sp", esink[:], sinks_d[0:1, :].to_broadcast([P, 8]), esink_t, writes=[esink_t])
        T.op("act", lambda: nc.scalar.activation(out=esink[:], in_=esink[:], func=AF.Exp),
             reads=[esink_t], writes=[esink_t])
        g1b, g1b_t = sb(top, "g1b", [P, D], F32)
        g2b, g2b_t = sb(top, "g2b", [P, D], F32)
        T.dma("sp", g1b[:], bada_row[0:1, 2048:3072].to_broadcast([P, D]), g1b_t, writes=[g1b_t])
        T.dma("sp", g2b[:], bada_row[0:1, 5120:6144].to_broadcast([P, D]), g2b_t, writes=[g2b_t])

        with ExitStack() as s0:
            cc, cc_t = sb(s0, "cc", [P, 8], F32)
            T.dma("sp", cc[:], c_col[:, :], cc_t, writes=[cc_t])
            scf, scf_t = sb(s0, "scf", [P, 8], F32)
            T.op("act", lambda: nc.scalar.activation(out=scf[:], in_=cc[:], func=AF.Silu),
                 reads=[cc_t], writes=[scf_t])
            scb16, scb16_t = sb(s0, "scb16", [P, 8], BF16)
            T.op("dve", lambda: nc.vector.tensor_copy(out=scb16[:], in_=scf[:]),
                 reads=[scf_t], writes=[scb16_t])
            scbc, scbc_t = sb(s0, "scbc", [P, 8, P], BF16)
            for kc in range(8):
                T.op("dve", lambda kc=kc: nc.vector.tensor_copy(
                    out=scbc[:, kc, :], in_=scf[:, kc:kc + 1].to_broadcast([P, P])),
                    reads=[scf_t], writes=[scbc_t])
            wv_bufs = [sb(s0, f"wadab{i}", [P, 8, 1024], BF16) for i in range(2)]
            for v in range(6):
                wv, wv_t = wv_bufs[v % 2]
                T.dma("pool", wv[:], wada[:, v * 8192:(v + 1) * 8192].rearrange(
                    "p (k c) -> p k c", k=8), wv_t, writes=[wv_t])
                if v in (0, 1, 3, 4):
                    b = bank()

                    def mm(b=b, wv=wv):
                        ins = None
                        for oc in range(8):
                            for kc in range(8):
                                ins = nc.tensor.matmul(ps[b][:, oc:oc + 1],
                                                       lhsT=wv[:, kc, oc * 128:(oc + 1) * 128],
                                                       rhs=scb16[:, kc:kc + 1],
                                                       start=(kc == 0), stop=(kc == 7))
                        return ins
                    T.op("pe", mm, reads=[wv_t, scb16_t], writes=[pst[b]])
                    T.op("dve", lambda b=b, v=v: nc.vector.tensor_tensor(
                        out=modcol[:, v * 8:(v + 1) * 8], in0=ps[b][:, 0:8],
                        in1=bcol[:, v * 8:(v + 1) * 8], op=ALU.add),
                        reads=[bcol_t], writes=[pst[b], modcol_t])
                    if v in (1, 4):
                        T.op("dve", lambda v=v: nc.vector.tensor_scalar(
                            out=modcol[:, v * 8:(v + 1) * 8], in0=modcol[:, v * 8:(v + 1) * 8],
                            scalar1=1.0, scalar2=None, op0=ALU.add),
                            reads=[modcol_t], writes=[modcol_t])
                else:
                    gX, gX_t = (g1b, g1b_t) if v == 2 else (g2b, g2b_t)
                    for h in range(2):
                        b = bank()

                        def mm(b=b, wv=wv, h=h):
                            ins = None
                            for kc in range(8):
                                ins = nc.tensor.matmul(ps[b][:, :], lhsT=scbc[:, kc, :],
                                                       rhs=wv[:, kc, h * 512:(h + 1) * 512],
                                                       start=(kc == 0), stop=(kc == 7))
                            return ins
                        T.op("pe", mm, reads=[wv_t, scbc_t], writes=[pst[b]])
                        T.op("dve", lambda b=b, h=h, gX=gX: nc.vector.tensor_tensor(
                            out=gX[:, h * 512:(h + 1) * 512], in0=ps[b][:, :],
                            in1=gX[:, h * 512:(h + 1) * 512], op=ALU.add),
                            reads=[gX_t], writes=[pst[b], gX_t])
            T.barrier()
        chk(1)

        def build_uT(stack, uT, uT_t, src_ap, src_tile, sc0, bi0, ntok):
            xb = [sb(stack, f"x4_{i}", [P, 4, D], F32) for i in range(2)]
            for s4 in range(ntok // 512):
                xt, xt_t = xb[s4 % 2]
                T.dma("sp", xt[:], src_ap[s4 * 512:(s4 + 1) * 512, :].rearrange(
                    "(t p) d -> p t d", p=P), xt_t, reads=[src_tile], writes=[xt_t])
                for kc in range(8):
                    b = bank()

                    def tr(b=b, xt=xt, kc=kc):
                        ins = None
                        for t in range(4):
                            ins = nc.tensor.transpose(out=ps[b][:, t * 128:(t + 1) * 128],
                                                      in_=xt[:, t, kc * 128:(kc + 1) * 128],
                                                      identity=ident[:])
                        return ins
                    T.op("pe", tr, reads=[xt_t, ident_t], writes=[pst[b]])
                    T.op("act", lambda b=b, kc=kc, s4=s4: nc.scalar.activation(
                        out=uT[:, kc, s4 * 512:(s4 + 1) * 512], in_=ps[b][:, :],
                        func=AF.Identity, scale=modcol[:, sc0 + kc:sc0 + kc + 1],
                        bias=modcol[:, bi0 + kc:bi0 + kc + 1]),
                        reads=[modcol_t], writes=[pst[b], uT_t])

        evac_flip = [0]

        def evac_copy(out_ap, in_ap, b, out_tile):
            e = "act" if evac_flip[0] % 2 == 0 else "dve"
            evac_flip[0] += 1
            if e == "act":
                T.op("act", lambda: nc.scalar.copy(out=out_ap, in_=in_ap),
                     writes=[pst[b], out_tile])
            else:
                T.op("dve", lambda: nc.vector.tensor_copy(out=out_ap, in_=in_ap),
                     writes=[pst[b], out_tile])

        def ln_tile(xt, xt_t, psb, gB, gB_t, lng, lng_t, lnb, lnb_t, tmp, tmp_t, st, st_t):
            for h in range(2):
                T.op("dve", lambda h=h: nc.vector.tensor_tensor(
                    out=tmp[:, h * 512:(h + 1) * 512], in0=ps[psb[h]][:, :],
                    in1=gB[:, h * 512:(h + 1) * 512], op=ALU.mult),
                    reads=[gB_t], writes=[pst[psb[h]], tmp_t])
            T.op("dve", lambda: nc.vector.scalar_tensor_tensor(
                out=xt, in0=xt, scalar=ALPHA, in1=tmp[:, :], op0=ALU.mult, op1=ALU.add),
                reads=[tmp_t, xt_t], writes=[xt_t])
            for h in range(2):
                T.op("dve", lambda h=h: nc.vector.bn_stats(
                    out=st[:, h * 6:(h + 1) * 6], in_=xt[:, h * 512:(h + 1) * 512]),
                    reads=[xt_t], writes=[st_t])
            T.op("dve", lambda: nc.vector.bn_aggr(out=st[:, 12:14], in_=st[:, 0:12]),
                 reads=[st_t], writes=[st_t])
            T.op("dve", lambda: nc.vector.tensor_scalar(
                out=st[:, 15:16], in0=st[:, 13:14], scalar1=EPS, scalar2=None,
                op0=ALU.add), reads=[st_t], writes=[st_t])
            T.op("act", lambda: nc.scalar.activation(
                out=st[:, 15:16], in_=st[:, 15:16], func=AF.Sqrt), reads=[st_t], writes=[st_t])
            T.op("dve", lambda: nc.vector.reciprocal(out=st[:, 14:15], in_=st[:, 15:16]),
                 reads=[st_t], writes=[st_t])
            T.op("dve", lambda: nc.vector.tensor_scalar(
                out=xt, in0=xt, scalar1=st[:, 12:13], scalar2=st[:, 14:15],
                op0=ALU.subtract, op1=ALU.mult), reads=[st_t, xt_t], writes=[xt_t])
            T.op("pool", lambda: nc.gpsimd.tensor_tensor(out=xt, in0=xt, in1=lng[:, :], op=ALU.mult),
                 reads=[lng_t, xt_t], writes=[xt_t])
            T.op("pool", lambda: nc.gpsimd.tensor_tensor(out=xt, in0=xt, in1=lnb[:, :], op=ALU.add),
                 reads=[lnb_t, xt_t], writes=[xt_t])

        with ExitStack() as smix:
            for grp in range(2):
                own_ap = x_in[grp * TG:(grp + 1) * TG, :]
                halo_ap = x_halo if grp == 0 else x_in[0:TG, :]
                with ExitStack() as sg:
                    uTo, uTo_t = sb(sg, "uTo", [P, 8, TG], BF16)
                    yaT, yaT_t = sb(sg, "yaT", [64, 8, TG], BF16)
                    ybT, ybT_t = sb(sg, "ybT", [64, 4, TG], BF16)
                    with ExitStack() as sh:
                        uTh, uTh_t = sb(sh, "uTh", [P, 8, TG], BF16)
                        etab, etab_t = sb(sh, "etab", [P, 10, 512], BF16)
                        T.dma("pool", etab[:], etab_d[:, :].rearrange("p (u c) -> p u c", u=10),
                              etab_t, writes=[etab_t])
                        with ExitStack() as sx:
                            build_uT(sx, uTo, uTo_t, own_ap, dram_in, 8, 0, TG)
                            build_uT(sx, uTh, uTh_t, halo_ap, dram_in, 8, 0, TG)
                            T.barrier()
                            chk(2)
                        qT, qT_t = sb(sh, "qT", [P, TG], BF16)
                        kT, kT_t = sb(sh, "kT", [P, 2 * TG], BF16)
                        V, V_t = sb(sh, "V", [P, 32, 2, 128], BF16)
                        pexp = [sb(sh, f"pexp{i}", [P, 512], BF16) for i in range(2)]
                        pm = [sb(sh, f"pm{i}", [P, 512], BF16) for i in range(2)]
                        accn = [sb(sh, f"accn{i}", [64, TG], F32) for i in range(2)]
                        accd = [sb(sh, f"accd{i}", [64, TG], F32) for i in range(2)]
                        dtmp = [sb(sh, f"dtmp{i}", [64, 256], F32) for i in range(2)]
                        wbufs = [sb(sh, f"wblk{i}", [P, 8, 128], BF16) for i in range(4)]
                        wrot = [0]

                        def load_w(blk):
                            w, w_t = wbufs[wrot[0] % len(wbufs)]
                            wrot[0] += 1
                            T.dma("pool", w[:], win[:, blk * 1024:(blk + 1) * 1024].rearrange(
                                "p (k c) -> p k c", k=8), w_t, writes=[w_t])
                            return w, w_t

                        def proj_fm(dst, dst_t, col0, w, w_t, uT, uT_t, tok0, ntok):
                            n512 = max(1, ntok // 512)
                            wdt = min(512, ntok)
                            for n in range(n512):
                                b = bank()

                                def mm(b=b, n=n):
                                    ins = None
                                    for kc in range(8):
                                        ins = nc.tensor.matmul(
                                            ps[b][:, 0:wdt], lhsT=w[:, kc, :],
                                            rhs=uT[:, kc, tok0 + n * 512:tok0 + n * 512 + wdt],
                                            start=(kc == 0), stop=(kc == 7))
                                    return ins
                                T.op("pe", mm, reads=[w_t, uT_t], writes=[pst[b]])
                                evac_copy(dst[:, col0 + n * 512:col0 + n * 512 + wdt],
                                          ps[b][:, 0:wdt], b, dst_t)

                        def proj_v(blk0, tok_slices, w, w_t, uT, uT_t):
                            nb = len(tok_slices)
                            for j0 in range(0, nb, 4):
                                nj = min(4, nb - j0)
                                b = bank()

                                def mm(b=b, j0=j0, nj=nj):
                                    ins = None
                                    for j in range(nj):
                                        sl = tok_slices[j0 + j]
                                        for kc in range(8):
                                            ins = nc.tensor.matmul(
                                                ps[b][:, j * 128:(j + 1) * 128],
                                                lhsT=uT[:, kc, sl], rhs=w[:, kc, :],
                                                start=(kc == 0), stop=(kc == 7))
                                    return ins
                                T.op("pe", mm, reads=[w_t, uT_t], writes=[pst[b]])
                                T.op("dve", lambda b=b, j0=j0, nj=nj: nc.vector.tensor_copy(
                                    out=V[:, blk0 + j0:blk0 + j0 + nj, :, 0:64],
                                    in_=ps[b][:, 0:nj * 128].rearrange("p (j h c) -> p j h c",
                                                                       j=nj, h=2)),
                                    writes=[pst[b], V_t])

                        def build_kv(d, kblk, vblk):
                            FB = 128 * d
                            nbo = TG // FB
                            wk, wk_t = load_w(kblk)
                            wv, wv_t = load_w(vblk)
                            T.op("dve", lambda: nc.vector.memset(V[:, :, :, 64:128], 1.0),
                                 writes=[V_t])
                            proj_fm(kT, kT_t, 0, wk, wk_t, uTh, uTh_t, TG - FB, FB)
                            proj_fm(kT, kT_t, FB, wk, wk_t, uTo, uTo_t, 0, TG)
                            hs = [slice(TG - FB + r, TG - FB + r + 127 * d + 1, d) for r in range(d)]
                            proj_v(0, hs, wv, wv_t, uTh, uTh_t)
                            os_ = [slice(FB * J + r, FB * J + r + 127 * d + 1, d)
                                   for J in range(nbo) for r in range(d)]
                            proj_v(d, os_, wv, wv_t, uTo, uTo_t)
                            if grp == 0:
                                T.op("dve", lambda: nc.vector.tensor_scalar(
                                    out=V[:, 0:d, :, :], in0=V[:, 0:d, :, :], scalar1=hv[:, 0:1],
                                    scalar2=None, op0=ALU.mult), reads=[hv_t, V_t], writes=[V_t])

                        def attend(ui, d, first):
                            FB = 128 * d
                            for i in range(16):
                                J, r = i // d, i % d
                                a = i % 2
                                qs = slice(FB * J + r, FB * J + r + 127 * d + 1, d)
                                kp = slice(FB * J + r, FB * J + r + 127 * d + 1, d)
                                kc_ = slice(FB * (J + 1) + r, FB * (J + 1) + r + 127 * d + 1, d)
                                bp, bc = J * d + r, (J + 1) * d + r
                                if a == 0:
                                    sbk = (bank(), bank())
                                    pair = []
                                pair.append((bp, bc))
                                for hh in range(2):
                                    b = sbk[hh]
                                    pr = slice(64 * hh, 64 * hh + 64)

                                    def mm(b=b, pr=pr):
                                        nc.tensor.matmul(ps[b][:, a * 256:a * 256 + 128],
                                                         lhsT=kT[pr, kp], rhs=qT[pr, qs],
                                                         start=True, stop=True)
                                        return nc.tensor.matmul(
                                            ps[b][:, a * 256 + 128:a * 256 + 256],
                                            lhsT=kT[pr, kc_], rhs=qT[pr, qs],
                                            start=True, stop=True)
                                    T.op("pe", mm, reads=[kT_t, qT_t], writes=[pst[b]])
                                if a == 0:
                                    continue
                                pms = []
                                for hh in range(2):
                                    b = sbk[hh]
                                    pe_, pe_t = pexp[hh]
                                    pm_, pm_t = pm[hh]
                                    T.op("act", lambda: nc.scalar.activation(
                                        out=pe_[:, :], in_=ps[b][:, :], func=AF.Exp, scale=0.125),
                                        writes=[pst[b], pe_t])
                                    for a2 in range(2):
                                        T.op("dve", lambda: nc.vector.tensor_tensor(
                                            out=pm_[:, a2 * 256:(a2 + 1) * 256],
                                            in0=pe_[:, a2 * 256:(a2 + 1) * 256],
                                            in1=etab[:, ui, hh * 256:(hh + 1) * 256], op=ALU.mult),
                                            reads=[pe_t, etab_t], writes=[pm_t])
                                    pms.append((pm_, pm_t))
                                ob = bank()

                                def pv():
                                    ins = None
                                    for hh in range(2):
                                        pm_ = pms[hh][0]
                                        for a2 in range(2):
                                            bp2, bc2 = pair[a2]
                                            o = ps[ob][:, (hh * 2 + a2) * 128:(hh * 2 + a2 + 1) * 128]
                                            nc.tensor.matmul(o, lhsT=V[:, bp2, hh, :],
                                                             rhs=pm_[:, a2 * 256:a2 * 256 + 128],
                                                             start=True, stop=False)
                                            ins = nc.tensor.matmul(
                                                o, lhsT=V[:, bc2, hh, :],
                                                rhs=pm_[:, a2 * 256 + 128:a2 * 256 + 256],
                                                start=False, stop=True)
                                    return ins
                                T.op("pe", pv, reads=[V_t, pms[0][1], pms[1][1]], writes=[pst[ob]])
                                if i % 2 == 1:
                                    i0 = i - 1
                                    J0, r0 = i0 // d, i0 % d
                                    for hh in range(2):
                                        an_, an_t = accn[hh]
                                        ad_, ad_t = accd[hh]

                                        def dstv(a_):
                                            if d == 1:
                                                return a_[:, 128 * i0:128 * i0 + 256].rearrange(
                                                    "p (a j) -> p a j", a=2)
                                            return a_[:, FB * J0:FB * (J0 + 1)].rearrange(
                                                "p (j r) -> p r j", r=d)[:, r0:r0 + 2, :]
                                        srcn = ps[ob][0:64, hh * 256:(hh + 1) * 256].rearrange(
                                            "p (a j) -> p a j", a=2)
                                        srcd = ps[ob][64:128, hh * 256:(hh + 1) * 256].rearrange(
                                            "p (a j) -> p a j", a=2)
                                        if first:
                                            T.op("act", lambda: nc.scalar.activation(
                                                out=dstv(an_), in_=srcn, func=AF.Identity),
                                                writes=[pst[ob], an_t])
                                            T.op("dve", lambda: nc.vector.tensor_scalar(
                                                out=dstv(ad_), in0=srcd, scalar1=0.0, scalar2=None,
                                                op0=ALU.add), writes=[pst[ob], ad_t])
                                        else:
                                            dt_, dt_t = dtmp[hh]
                                            T.op("act", lambda: nc.scalar.activation(
                                                out=dt_[:, :].rearrange("p (a j) -> p a j", a=2),
                                                in_=srcd, func=AF.Identity),
                                                writes=[pst[ob], dt_t])
                                            T.op("dve", lambda: nc.vector.tensor_tensor(
                                                out=dstv(an_), in0=srcn, in1=dstv(an_), op=ALU.add),
                                                reads=[an_t], writes=[pst[ob], an_t])
                                            T.op("pool", lambda: nc.gpsimd.tensor_tensor(
                                                out=dstv(ad_), in0=dstv(ad_),
                                                in1=dt_[:, :].rearrange("p (a j) -> p a j", a=2),
                                                op=ALU.add), reads=[dt_t, ad_t], writes=[ad_t])

                        def finalize(dstT, dstT_t, slots, sink_heads):
                            for hh in range(2):
                                an_, an_t = accn[hh]
                                ad_, ad_t = accd[hh]
                                if sink_heads is not None:
                                    h = sink_heads[hh]
                                    T.op("dve", lambda: nc.vector.tensor_scalar(
                                        out=ad_[:, :], in0=ad_[:, :],
                                        scalar1=esink[0:64, h:h + 1], scalar2=None, op0=ALU.add),
                                        reads=[ad_t, esink_t], writes=[ad_t])
                                T.op("dve", lambda: nc.vector.reciprocal(
                                    out=ad_[:, :], in_=ad_[:, :]),
                                    reads=[ad_t], writes=[ad_t])
                                T.op("dve", lambda: nc.vector.tensor_tensor(
                                    out=dstT[:, slots[hh], :], in0=an_[:, :], in1=ad_[:, :],
                                    op=ALU.mult), reads=[an_t, ad_t], writes=[dstT_t])

                        build_kv(1, 0, 1)
                        chk(3)
                        for u in range(4):
                            wq, wq_t = load_w(2 + u)
                            proj_fm(qT, qT_t, 0, wq, wq_t, uTo, uTo_t, 0, TG)
                            attend(u, 1, True)
                            finalize(yaT, yaT_t, (u, u + 4), (u, u + 4))
                            chk(4)
                        for hp in range(2):
                            for g in range(3):
                                d = B_GROUPS[g][1]
                                blk = 6 + (hp * 3 + g) * 3
                                build_kv(d, blk + 1, blk + 2)
                                wq, wq_t = load_w(blk)
                                proj_fm(qT, qT_t, 0, wq, wq_t, uTo, uTo_t, 0, TG)
                                attend(4 + hp * 3 + g, d, g == 0)
                            finalize(ybT, ybT_t, (2 * hp, 2 * hp + 1), None)
                            chk(5)
                        T.barrier()
                    with ExitStack() as sm:
                        mT, mT_t = sb(sm, "mT", [P, 8, TG], BF16)
                        ln1g, ln1g_t = sb(sm, "ln1g", [P, D], F32)
                        ln1b, ln1b_t = sb(sm, "ln1b", [P, D], F32)
                        T.dma("sp", ln1g[:], ln1g_d[0:1, :].to_broadcast([P, D]), ln1g_t,
                              writes=[ln1g_t])
                        T.dma("sp", ln1b[:], ln1b_d[0:1, :].to_broadcast([P, D]), ln1b_t,
                              writes=[ln1b_t])
                        wa, wa_t = sb(sm, "wa", [64, 8, D], BF16)
                        wb_, wb_t = sb(sm, "wb", [64, 4, D], BF16)
                        wo, wo_t = sb(sm, "wo", [P, 8, D], BF16)
                        T.dma("pool", wa[:], wa_d[:, :].rearrange("p (k c) -> p k c", k=8), wa_t,
                              writes=[wa_t])
                        T.dma("pool", wb_[:], wb_d[:, :].rearrange("p (k c) -> p k c", k=4), wb_t,
                              writes=[wb_t])
                        T.dma("pool", wo[:], wo_d[:, :].rearrange("p (k c) -> p k c", k=8), wo_t,
                              writes=[wo_t])
                        gbufs = [sb(sm, f"wgate{i}", [P, 8, 128], BF16) for i in range(4)]
                        sga = [sb(sm, f"sga{i}", [P, 512], F32) for i in range(1)]
                        sgb = [sb(sm, f"sgb{i}", [P, 512], F32) for i in range(1)]
                        t1 = [sb(sm, f"t1_{i}", [P, 512], F32) for i in range(1)]
                        t2 = [sb(sm, f"t2_{i}", [P, 512], F32) for i in range(1)]
                        it = 0
                        for c in range(8):
                            wga, wga_t = gbufs[(2 * c) % 4]
                            wgb, wgb_t = gbufs[(2 * c + 1) % 4]
                            T.dma("pool", wga[:], win[:, (24 + 2 * c) * 1024:(25 + 2 * c) * 1024]
                                  .rearrange("p (k c) -> p k c", k=8), wga_t, writes=[wga_t])
                            T.dma("pool", wgb[:], win[:, (25 + 2 * c) * 1024:(26 + 2 * c) * 1024]
                                  .rearrange("p (k c) -> p k c", k=8), wgb_t, writes=[wgb_t])
                            for n in range(4):
                                ts_ = slice(n * 512, (n + 1) * 512)
                                bA, bB, bGa, bGb = bank(), bank(), bank(), bank()

                                def mmA(bA=bA, c=c, ts_=ts_):
                                    ins = None
                                    for k in range(8):
                                        ins = nc.tensor.matmul(ps[bA][:, :],
                                                               lhsT=wa[:, k, c * 128:(c + 1) * 128],
                                                               rhs=yaT[:, k, ts_],
                                                               start=(k == 0), stop=(k == 7))
                                    return ins
                                T.op("pe", mmA, reads=[wa_t, yaT_t], writes=[pst[bA]])

                                def mmB(bB=bB, c=c, ts_=ts_):
                                    ins = None
                                    for k in range(4):
                                        ins = nc.tensor.matmul(ps[bB][:, :],
                                                               lhsT=wb_[:, k, c * 128:(c + 1) * 128],
                                                               rhs=ybT[:, k, ts_],
                                                               start=(k == 0), stop=(k == 3))
                                    return ins
                                T.op("pe", mmB, reads=[wb_t, ybT_t], writes=[pst[bB]])

                                def mmG(bX, w, ts_=ts_):
                                    ins = None
                                    for kc in range(8):
                                        ins = nc.tensor.matmul(ps[bX][:, :], lhsT=w[:, kc, :],
                                                               rhs=uTo[:, kc, ts_],
                                                               start=(kc == 0), stop=(kc == 7))
                                    return ins
                                T.op("pe", lambda: mmG(bGa, wga), reads=[wga_t, uTo_t],
                                     writes=[pst[bGa]])
                                T.op("pe", lambda: mmG(bGb, wgb), reads=[wgb_t, uTo_t],
                                     writes=[pst[bGb]])
                                sa, sa_t = sga[0]
                                sb2, sb2_t = sgb[0]
                                x1_, x1_t = t1[0]
                                x2_, x2_t = t2[0]
                                it += 1
                                T.op("act", lambda: nc.scalar.activation(
                                    out=sa[:, :], in_=ps[bGa][:, :], func=AF.Sigmoid),
                                    writes=[pst[bGa], sa_t])
                                T.op("act", lambda: nc.scalar.activation(
                                    out=sb2[:, :], in_=ps[bGb][:, :], func=AF.Sigmoid),
                                    writes=[pst[bGb], sb2_t])
                                T.op("dve", lambda: nc.vector.tensor_tensor(
                                    out=x1_[:, :], in0=ps[bA][:, :], in1=sa[:, :], op=ALU.mult),
                                    reads=[sa_t], writes=[pst[bA], x1_t])
                                T.op("dve", lambda: nc.vector.tensor_tensor(
                                    out=x2_[:, :], in0=ps[bB][:, :], in1=sb2[:, :], op=ALU.mult),
                                    reads=[sb2_t], writes=[pst[bB], x2_t])
                                T.op("pool", lambda: nc.gpsimd.tensor_tensor(
                                    out=mT[:, c, ts_], in0=x1_[:, :], in1=x2_[:, :], op=ALU.add),
                                    reads=[x1_t, x2_t], writes=[mT_t])
                        chk(6)
                        xts = [sb(sm, f"xt{i}", [P, D], F32) for i in range(2)]
                        tmps = [sb(sm, f"lntmp{i}", [P, D], F32) for i in range(1)]
                        sts = [sb(sm, f"lnst{i}", [P, 16], F32) for i in range(2)]
                        for tt in range(16):
                            row0 = grp * TG + tt * 128
                            xt, xt_t = xts[tt % 2]
                            T.dma("sp", xt[:], x_in[row0:row0 + 128, :], xt_t, reads=[dram_in],
                                  writes=[xt_t])
                            b0, b1 = bank(), bank()
                            for h, bb in ((0, b0), (1, b1)):
                                def mmo(bb=bb, h=h, tt=tt):
                                    ins = None
                                    for k in range(8):
                                        ins = nc.tensor.matmul(
                                            ps[bb][:, :], lhsT=mT[:, k, tt * 128:(tt + 1) * 128],
                                            rhs=wo[:, k, h * 512:(h + 1) * 512],
                                            start=(k == 0), stop=(k == 7))
                                    return ins
                                T.op("pe", mmo, reads=[mT_t, wo_t], writes=[pst[bb]])
                            tmp, tmp_t = tmps[0]
                            st, st_t = sts[tt % 2]
                            ln_tile(xt[:, :], xt_t, (b0, b1), g1b, g1b_t, ln1g, ln1g_t, ln1b, ln1b_t,
                                    tmp, tmp_t, st, st_t)
                            T.dma("sp", x1_d[row0:row0 + 128, :], xt[:], xt_t, reads=[xt_t],
                                  writes=[dram_x1])
                        T.barrier()
                        chk(7)
            T.barrier()

        with ExitStack() as sf:
            wg, wg_t = sb(sf, "wg", [P, 8, DFF], BF16)
            wu, wu_t = sb(sf, "wu", [P, 8, DFF], BF16)
            wd, wd_t = sb(sf, "wd", [P, NF, D], BF16)
            for k in range(8):
                T.dma("pool", wg[:, k, :], wg_d[:, k * DFF:(k + 1) * DFF], wg_t, writes=[wg_t])
            for k in range(8):
                T.dma("pool", wu[:, k, :], wu_d[:, k * DFF:(k + 1) * DFF], wu_t, writes=[wu_t])
            for k in range(0, NF, 2):
                T.dma("pool", wd[:, k:k + 2, :], wd_d[:, k * D:(k + 2) * D].rearrange(
                    "p (k c) -> p k c", k=2), wd_t, writes=[wd_t])
            ln2g, ln2g_t = sb(sf, "ln2g", [P, D], F32)
            ln2b, ln2b_t = sb(sf, "ln2b", [P, D], F32)
            T.dma("sp", ln2g[:], ln2g_d[0:1, :].to_broadcast([P, D]), ln2g_t, writes=[ln2g_t])
            T.dma("sp", ln2b[:], ln2b_d[0:1, :].to_broadcast([P, D]), ln2b_t, writes=[ln2b_t])
            x4, x4_t = sb(sf, "fx4", [P, 4, D], F32)
            u2, u2_t = sb(sf, "u2T", [P, 8, 512], BF16)
            hT, hT_t = sb(sf, "hT", [P, NF, 512], BF16)
            sgs = [sb(sf, f"fsg{i}", [P, 512], F32) for i in range(2)]
            tmps = [sb(sf, f"flntmp{i}", [P, D], F32) for i in range(1)]
            sts = [sb(sf, f"flnst{i}", [P, 16], F32) for i in range(2)]
            for sgi in range(T_OWN // 512):
                r0 = sgi * 512
                T.dma("sp", x4[:], x1_d[r0:r0 + 512, :].rearrange("(t p) d -> p t d", p=P), x4_t,
                      reads=[dram_x1], writes=[x4_t])
                for kc in range(8):
                    b = bank()

                    def tr(b=b, kc=kc):
                        ins = None
                        for t in range(4):
                            ins = nc.tensor.transpose(out=ps[b][:, t * 128:(t + 1) * 128],
                                                      in_=x4[:, t, kc * 128:(kc + 1) * 128],
                                                      identity=ident[:])
                        return ins
                    T.op("pe", tr, reads=[x4_t, ident_t], writes=[pst[b]])
                    T.op("act", lambda b=b, kc=kc: nc.scalar.activation(
                        out=u2[:, kc, :], in_=ps[b][:, :], func=AF.Identity,
                        scale=modcol[:, 32 + kc:33 + kc], bias=modcol[:, 24 + kc:25 + kc]),
                        reads=[modcol_t], writes=[pst[b], u2_t])
                for f in range(NF):
                    bg, bu = bank(), bank()

                    def mmf(bX, w, f=f):
                        ins = None
                        for kc in range(8):
                            ins = nc.tensor.matmul(ps[bX][:, :], lhsT=w[:, kc, f * 128:(f + 1) * 128],
                                                   rhs=u2[:, kc, :], start=(kc == 0), stop=(kc == 7))
                        return ins
                    T.op("pe", lambda: mmf(bg, wg), reads=[wg_t, u2_t], writes=[pst[bg]])
                    T.op("pe", lambda: mmf(bu, wu), reads=[wu_t, u2_t], writes=[pst[bu]])
                    sg_, sg_t = sgs[f % 2]
                    T.op("act", lambda: nc.scalar.activation(out=sg_[:, :], in_=ps[bg][:, :],
                                                             func=AF.Silu),
                         writes=[pst[bg], sg_t])
                    T.op("dve", lambda f=f: nc.vector.tensor_tensor(
                        out=hT[:, f, :], in0=ps[bu][:, :], in1=sg_[:, :], op=ALU.mult),
                        reads=[sg_t], writes=[pst[bu], hT_t])
                for tt in range(4):
                    b0, b1 = bank(), bank()
                    for h, bb in ((0, b0), (1, b1)):
                        def mmd(bb=bb, h=h, tt=tt):
                            ins = None
                            for f in range(NF):
                                ins = nc.tensor.matmul(ps[bb][:, :],
                                                       lhsT=hT[:, f, tt * 128:(tt + 1) * 128],
                                                       rhs=wd[:, f, h * 512:(h + 1) * 512],
                                                       start=(f == 0), stop=(f == NF - 1))
                            return ins
                        T.op("pe", mmd, reads=[hT_t, wd_t], writes=[pst[bb]])
                    tmp, tmp_t = tmps[0]
                    st, st_t = sts[tt % 2]
                    ln_tile(x4[:, tt, :], x4_t, (b0, b1), g2b, g2b_t, ln2g, ln2g_t, ln2b, ln2b_t,
                            tmp, tmp_t, st, st_t)
                T.dma("sp", x_out[r0:r0 + 512, :].rearrange("(t p) d -> p t d", p=P), x4[:], x4_t,
                      reads=[x4_t], writes=[dram_out])
            T.barrier()


def _kmajor(w, nk):
    n = w.shape[1]
    return np.ascontiguousarray(w.reshape(nk, P, n).transpose(1, 0, 2).reshape(P, nk * n))


def _etab():
    slopes = np.exp2(-8.0 * np.arange(1, 21, dtype=np.float64) / 20)
    s = np.arange(128)[:, None].astype(np.float64)
    q = np.arange(128)[None, :].astype(np.float64)
    out = np.zeros((P, 10, 2, 2, 128), np.float32)
    for ui, u in enumerate(UNITS):
        for hh in range(2):
            sl = slopes[u["slope_idx"][hh]] * u["d"]
            dist_c = q - s
            dist_p = q + 128 - s
            e_c = np.where(dist_c >= 0, np.exp(-sl * np.maximum(dist_c, 0)), 0.0)
            e_p = np.where(dist_p <= u["maxdist"], np.exp(-sl * dist_p), 0.0)
            out[:, ui, hh, 0, :] = e_p
            out[:, ui, hh, 1, :] = e_c
    return out.reshape(P, 10 * 512)


_NC_CACHE = {}


def _layer_consts(l, w_ada, b_ada, w_in, sinks, w_a, w_b, w_o, ln1_g, ln1_b, w_gate, w_up, w_down,
                  ln2_g, ln2_b):
    d = {}
    wada = w_ada[l].reshape(8, P, 6, 1024).transpose(1, 2, 0, 3)
    d["wada"] = np.ascontiguousarray(wada.reshape(P, 6 * 8 * 1024))
    d["bada_col"] = np.ascontiguousarray(b_ada[l].reshape(6, 8, P).transpose(2, 0, 1).reshape(P, 48))
    d["bada_row"] = np.ascontiguousarray(b_ada[l].reshape(1, 6144))
    cols = np.concatenate([np.asarray(b) for b in WIN_BLOCKS])
    wi = w_in[l][:, cols].reshape(8, P, NBLK, 128).transpose(1, 2, 0, 3)
    d["win"] = np.ascontiguousarray(wi.reshape(P, NBLK * 1024))
    d["wa"] = np.ascontiguousarray(w_a[l].reshape(8, 64, D).transpose(1, 0, 2).reshape(64, 8 * D))
    d["wb"] = np.ascontiguousarray(w_b[l].reshape(4, 64, D).transpose(1, 0, 2).reshape(64, 4 * D))
    d["wo"] = _kmajor(w_o[l], 8)
    d["wg"] = _kmajor(w_gate[l], 8)
    d["wu"] = _kmajor(w_up[l], 8)
    d["wd"] = _kmajor(w_down[l], NF)
    d["sinks"] = np.ascontiguousarray(sinks[l].reshape(1, 8))
    d["ln1g"] = np.ascontiguousarray(ln1_g[l].reshape(1, D))
    d["ln1b"] = np.ascontiguousarray(ln1_b[l].reshape(1, D))
    d["ln2g"] = np.ascontiguousarray(ln2_g[l].reshape(1, D))
    d["ln2b"] = np.ascontiguousarray(ln2_b[l].reshape(1, D))
    return d


def kernel(x, c, w_ada, b_ada, w_in, sinks, w_a, w_b, w_o, ln1_g, ln1_b,
           w_gate, w_up, w_down, ln2_g, ln2_b, _n_layers=DEPTH):
    args = [np.asarray(a, dtype=np.float32) for a in
            (x, c, w_ada, b_ada, w_in, sinks, w_a, w_b, w_o, ln1_g, ln1_b, w_gate, w_up, w_down,
             ln2_g, ln2_b)]
    (x, c, w_ada, b_ada, w_in, sinks, w_a, w_b, w_o, ln1_g, ln1_b, w_gate, w_up, w_down,
     ln2_g, ln2_b) = args
    if "nc" not in _NC_CACHE:
        _NC_CACHE["nc"] = build_program()
    nc = _NC_CACHE["nc"]
    etab = _etab()
    ident = np.eye(P, dtype=np.float32)
    xcur = np.ascontiguousarray(x)
    for l in range(_n_layers):
        lc = _layer_consts(l, w_ada, b_ada, w_in, sinks, w_a, w_b, w_o, ln1_g, ln1_b, w_gate, w_up,
                           w_down, ln2_g, ln2_b)
        in_maps = []
        for core in range(N_CORES):
            b, j = core // 4, core % 4
            m = dict(lc)
            m["x_in"] = np.ascontiguousarray(xcur[b, j * T_OWN:(j + 1) * T_OWN, :])
            if j == 0:
                m["x_halo"] = np.zeros((HALO, D), np.float32)
                m["hv"] = np.zeros((P, 1), np.float32)
            else:
                m["x_halo"] = np.ascontiguousarray(xcur[b, j * T_OWN - HALO:j * T_OWN, :])
                m["hv"] = np.ones((P, 1), np.float32)
            m["c_col"] = np.ascontiguousarray(c[b].reshape(8, P).T)
            m["etab"] = etab
            m["ident"] = ident
            in_maps.append(m)
        res = run_bass_kernel_spmd(nc, in_maps, core_ids=list(range(N_CORES)))
        xnew = np.empty_like(xcur)
        for core in range(N_CORES):
            b, j = core // 4, core % 4
            xnew[b, j * T_OWN:(j + 1) * T_OWN, :] = res.results[core]["x_out"]
        xcur = xnew
    return xcur
```
